# Optimizing a Trainium2 kernel written in Bass

```python
import jax, jax.numpy as jnp
from jax import lax
import numpy as np

D_MODEL = 1024
BATCH = 8
SEQ = 8192
DEPTH = 2

CHUNK = 64
N_MIXERS = 2
POOL_WINDOWS = (2, 4, 8, 16)
N_POOL_GROUPS = len(POOL_WINDOWS)
POOL_GROUP_DIM = D_MODEL // N_POOL_GROUPS
CONV_WIDTH = 3
D_FF = ((8 * D_MODEL // 3 + 255) // 256) * 256
N_SUB = 3
N_MOD = 3
RMS_EPS = 1e-6
N_POOL_LAYERS = (DEPTH + 1) // 2
N_CONV_LAYERS = DEPTH // 2

kernel_name = "hybrid_pool_shortconv_macaron_adaln"


def rms_norm(x, g):
    xf = x.astype(jnp.float32)
    y = xf * lax.rsqrt(jnp.mean(xf * xf, axis=-1, keepdims=True) + RMS_EPS)
    return (y * g.astype(jnp.float32)).astype(x.dtype)


def modulate(h, shift, scale):
    return h * (1 + scale[:, None, :]) + shift[:, None, :]


def swiglu(h, w_in, w_out):
    a, b = jnp.split(h @ w_in, 2, axis=-1)
    return (jax.nn.silu(a) * b) @ w_out


def pool_mixer(h, w_grp, scale):
    B, T, D = h.shape
    hg = h.reshape(B, T, N_POOL_GROUPS, POOL_GROUP_DIM)
    cs = jnp.cumsum(hg.astype(jnp.float32), axis=1)
    pos = jnp.arange(1, T + 1, dtype=jnp.float32)
    pooled = []
    for g, w in enumerate(POOL_WINDOWS):
        csg = cs[:, :, g]
        lag = jnp.pad(csg, ((0, 0), (w, 0), (0, 0)))[:, :T]
        cnt = jnp.minimum(pos, float(w))[None, :, None]
        pooled.append((csg - lag) / cnt)
    pooled = jnp.stack(pooled, axis=2).astype(h.dtype)
    mixed = pooled - hg
    y = jnp.einsum('btgc,gcd->btgd', mixed, w_grp).reshape(B, T, D)
    return y * scale


def conv_mixer(h, w_in, w_conv, w_out):
    D = h.shape[-1]
    b_gate, c_gate, v = jnp.split(h @ w_in, 3, axis=-1)
    u = c_gate * v
    conv = lax.conv_general_dilated(
        u, w_conv[:, None, :].astype(u.dtype),
        window_strides=(1,),
        padding=[(CONV_WIDTH - 1, 0)],
        dimension_numbers=('NWC', 'WIO', 'NWC'),
        feature_group_count=D)
    return (b_gate * conv) @ w_out


def setup_inputs(seed: int = 0) -> dict:
    key = jax.random.key(seed)
    ks = jax.random.split(key, 14)
    f32 = jnp.float32
    x = jax.random.normal(ks[0], (BATCH, SEQ, D_MODEL), f32)
    c = jax.random.normal(ks[1], (BATCH, D_MODEL), f32)
    ada_w = jax.random.normal(ks[2], (DEPTH, D_MODEL, N_SUB * N_MOD * D_MODEL), f32) * D_MODEL ** -0.5
    ada_b = 0.01 * jax.random.normal(ks[3], (DEPTH, N_SUB * N_MOD * D_MODEL), f32)
    norm_g = 1.0 + 0.1 * jax.random.normal(ks[4], (DEPTH, N_SUB, D_MODEL), f32)
    ffn_w_in = jax.random.normal(ks[5], (DEPTH, 2, D_MODEL, 2 * D_FF), f32) * D_MODEL ** -0.5
    ffn_w_out = jax.random.normal(ks[6], (DEPTH, 2, D_FF, D_MODEL), f32) * D_FF ** -0.5
    pool_w = jax.random.normal(ks[7], (N_POOL_LAYERS, N_POOL_GROUPS, POOL_GROUP_DIM, POOL_GROUP_DIM), f32) * POOL_GROUP_DIM ** -0.5
    pool_scale = 1.0 + 0.1 * jax.random.normal(ks[8], (N_POOL_LAYERS, D_MODEL), f32)
    conv_w_in = jax.random.normal(ks[9], (N_CONV_LAYERS, D_MODEL, 3 * D_MODEL), f32) * D_MODEL ** -0.5
    conv_w = jax.random.normal(ks[10], (N_CONV_LAYERS, CONV_WIDTH, D_MODEL), f32) * CONV_WIDTH ** -0.5
    conv_w_out = jax.random.normal(ks[11], (N_CONV_LAYERS, D_MODEL, D_MODEL), f32) * D_MODEL ** -0.5
    final_g = 1.0 + 0.1 * jax.random.normal(ks[12], (D_MODEL,), f32)
    return {"x": x, "c": c, "ada_w": ada_w, "ada_b": ada_b, "norm_g": norm_g,
            "ffn_w_in": ffn_w_in, "ffn_w_out": ffn_w_out,
            "pool_w": pool_w, "pool_scale": pool_scale,
            "conv_w_in": conv_w_in, "conv_w": conv_w, "conv_w_out": conv_w_out,
            "final_g": final_g}


def reference(x, c, ada_w, ada_b, norm_g, ffn_w_in, ffn_w_out, pool_w, pool_scale,
              conv_w_in, conv_w, conv_w_out, final_g):
    B = x.shape[0]
    cond = jax.nn.silu(c)
    for i in range(DEPTH):
        mod = (cond @ ada_w[i] + ada_b[i]).reshape(B, N_SUB, N_MOD, D_MODEL)
        shift, scale, gate = mod[:, :, 0], mod[:, :, 1], mod[:, :, 2]

        h = modulate(rms_norm(x, norm_g[i, 0]), shift[:, 0], scale[:, 0])
        x = x + 0.5 * gate[:, 0, None, :] * swiglu(h, ffn_w_in[i, 0], ffn_w_out[i, 0])

        h = modulate(rms_norm(x, norm_g[i, 1]), shift[:, 1], scale[:, 1])
        j = i // N_MIXERS
        if i % N_MIXERS == 0:
            y = pool_mixer(h, pool_w[j], pool_scale[j])
        else:
            y = conv_mixer(h, conv_w_in[j], conv_w[j], conv_w_out[j])
        x = x + gate[:, 1, None, :] * y

        h = modulate(rms_norm(x, norm_g[i, 2]), shift[:, 2], scale[:, 2])
        x = x + 0.5 * gate[:, 2, None, :] * swiglu(h, ffn_w_in[i, 1], ffn_w_out[i, 1])
    return rms_norm(x, final_g)
```

```python
import contextlib
import numpy as np
import concourse.bass as bass
import concourse.mybir as mybir
from concourse.bass_utils import run_bass_kernel_spmd

F32 = mybir.dt.float32
BF16 = mybir.dt.bfloat16
AF = mybir.ActivationFunctionType
ALU = mybir.AluOpType

D = 1024
DC = D // 128
DFF = 2816
FC = DFF // 128
SEQ = 8192
NB = 8
EPS = 1e-6
POOL_W = (2, 4, 8, 16)
SQRT_D = float(np.sqrt(np.float32(D)))

V_C = 0
V_ADAB = V_C + 8
V_NG = V_ADAB + 144
V_PS = V_NG + 48
V_CW = V_PS + 8
V_FG = V_CW + 24
NV = V_FG + 8


class SemC:
    def __init__(self, sem, unit):
        self.sem = sem
        self.unit = unit
        self.count = 0


class Buf:
    __slots__ = ("w", "r")

    def __init__(self):
        self.w = None
        self.r = {}


class Stream:
    def __init__(self, name, sc, is_pe=False):
        self.name = name
        self.sc = sc
        self.is_pe = is_pe
        self.items = []
        self.seen = {}


class Plan:
    def emit(self, st, meth, kw, reads=(), writes=(), sc=None, inc=True):
        fn = (meth, kw)
        deps = {}

        def need(s, c):
            if s is st.sc and st.is_pe:
                return
            if deps.get(s, 0) < c:
                deps[s] = c

        for b in reads:
            if b.w is not None:
                need(*b.w)
        for b in writes:
            if b.w is not None:
                need(*b.w)
            for s, c in b.r.items():
                need(s, c)
        for s, c in deps.items():
            if st.seen.get(s, 0) >= c:
                continue
            st.items.append(("wait", s.sem, c * s.unit))
            st.seen[s] = c
        tgt = sc if sc is not None else st.sc
        if inc:
            tgt.count += 1
            st.items.append(("op", fn, tgt.sem, tgt.unit))
            stamp = (tgt, tgt.count)
        else:
            st.items.append(("op", fn, None, 0))
            stamp = (tgt, tgt.count + 1)
        for b in reads:
            if b.r.get(stamp[0], 0) < stamp[1]:
                b.r[stamp[0]] = stamp[1]
        for b in writes:
            b.w = stamp
            b.r = {}


def replay(items, eng):
    for it in items:
        if it[0] == "wait":
            eng.wait_ge(it[1], it[2])
        else:
            ins = getattr(eng, it[1][0])(**it[1][1])
            if it[2] is not None:
                ins.then_inc(it[2], it[3])


def build_nc(T, TT, FS=2, n_layers=2, do_final=True, first=True, carry_io=False):
    S = TT // 512
    NT = T // TT
    assert T % TT == 0 and TT % 512 == 0
    base, rem = divmod(FC, FS)
    fsplits = []
    f0 = 0
    for q in range(FS):
        n = base + (1 if q < rem else 0)
        fsplits.append((f0, n))
        f0 += n
    NR = max(max(n for _, n in fsplits), DC)

    nc = bass.Bass("TRN2", target_bir_lowering=False)
    xT = nc.dram_tensor("xT", [D, T], F32, kind="ExternalInput").ap()
    vecs = nc.dram_tensor("vecs", [128, NV], F32, kind="ExternalInput").ap()
    adaw = nc.dram_tensor("adaw", [2 * 36, 128, 2 * 1024], F32, kind="ExternalInput").ap()
    win = nc.dram_tensor("win", [4 * FC, 128, 2048], F32, kind="ExternalInput").ap()
    wout = nc.dram_tensor("wout", [4 * DC, 128, FC * 128], F32, kind="ExternalInput").ap()
    poolw = nc.dram_tensor("poolw", [4, 128, 512], F32, kind="ExternalInput").ap()
    cwin = nc.dram_tensor("cwin", [DC, 128, 3072], F32, kind="ExternalInput").ap()
    cwout = nc.dram_tensor("cwout", [DC, 128, 1024], F32, kind="ExternalInput").ap()
    outT = nc.dram_tensor("outT", [D, T], F32, kind="ExternalOutput").ap()
    if carry_io:
        cin = nc.dram_tensor("cin", [128, DC * 18], F32, kind="ExternalInput").ap()
        cout = nc.dram_tensor("cout", [128, DC * 18], F32, kind="ExternalOutput").ap()
    xT3 = xT.rearrange("(c p) t -> p c t", p=128)
    outT3 = outT.rearrange("(c p) t -> p c t", p=128)

    es = contextlib.ExitStack()
    with es:
        def sb(name, shape, dt):
            return es.enter_context(nc.sbuf_tensor(name, shape, dt))

        def new_sem(name):
            return es.enter_context(nc.semaphore(name))

        X = sb("X", [128, DC, TT], F32)
        H = sb("H", [128, DC, TT], BF16)
        R = sb("R", [128, NR, TT], BF16)
        NWA, NWB = 3, 3
        WA = [sb(f"WA{i}", [128, 3072], BF16) for i in range(NWA)]
        WB = [sb(f"WB{i}", [128, 11 * 128], BF16) for i in range(NWB)]
        ADS = [sb(f"ADS{i}", [128, 2048], F32) for i in range(2)]
        V = sb("V", [128, NV], F32)
        ONES = sb("ONES", [128, 128], BF16)
        COND = sb("COND", [128, 8], F32)
        MOD = sb("MOD", [128, 2 * 72], F32)
        DER = sb("DER", [128, 2 * 3 * 2 * 8 + 8], F32)
        VT = sb("VT", [128, 8], F32)
        EPSV = sb("EPSV", [128, 1], F32)
        SQ = [sb(f"SQ{i}", [128, DC, 512], BF16) for i in range(2)]
        RSTD = [sb(f"RSTD{i}", [128, 512], F32) for i in range(2)]
        NTMP = 3
        TMP = [sb(f"TMP{i}", [128, 512], F32) for i in range(NTMP)]
        NSIL = 3
        SIL = [sb(f"SIL{i}", [128, 512], F32) for i in range(NSIL)]
        RSA = sb("RSA", [128, TT], F32)
        HP = [sb(f"HP{i}", [128, 16 + TT], F32) for i in range(2)]
        PL = [sb(f"PL{i}", [128, 16 + TT], F32) for i in range(2)]
        CARP = sb("CARP", [128, DC, 16], F32)
        INVC = sb("INVC", [128, 4, 16], F32)
        FIX = sb("FIX", [128, 16], F32)
        UE = [sb(f"UE{i}", [128, 2 + TT], F32) for i in range(2)]
        CARU = sb("CARU", [128, DC, 2], F32)
        PSB = [es.enter_context(nc.psum_tensor(f"PS{i}", [128, 512], F32)) for i in range(8)]

        pe = Stream("pe", SemC(new_sem("s_pe"), 1), is_pe=True)
        act = Stream("act", SemC(new_sem("s_act"), 1))
        dve = Stream("dve", SemC(new_sem("s_dve"), 1))
        pool = Stream("pool", SemC(new_sem("s_pool"), 1))
        sp = Stream("sp", SemC(new_sem("s_sp"), 1))
        P = Plan()

        bX = [[Buf() for _ in range(S)] for _ in range(DC)]
        bH = [[Buf() for _ in range(S)] for _ in range(DC)]
        bR = [[Buf() for _ in range(S)] for _ in range(NR)]
        bWA = [Buf() for _ in range(NWA)]
        bWB = [Buf() for _ in range(NWB)]
        bADS = [Buf() for _ in range(2)]
        bV, bONES, bCOND, bMOD, bDER, bVT = Buf(), Buf(), Buf(), Buf(), Buf(), Buf()
        bEPS = Buf()
        bSQ = [Buf() for _ in range(2)]
        bRSTD = [Buf() for _ in range(2)]
        bTMP = [Buf() for _ in range(NTMP)]
        bSIL = [Buf() for _ in range(NSIL)]
        bRSA = [Buf() for _ in range(S)]
        bHP = [Buf() for _ in range(2)]
        bPL = [Buf() for _ in range(2)]
        bCARP, bINVC, bFIX = Buf(), Buf(), Buf()
        bUE = [Buf() for _ in range(2)]
        bCARU = Buf()
        bPS = [Buf() for _ in range(8)]

        scWA = [SemC(new_sem(f"s_wa{i}"), 16) for i in range(NWA)]
        scWB = [SemC(new_sem(f"s_wb{i}"), 16) for i in range(NWB)]
        scADS = [SemC(new_sem(f"s_ads{i}"), 16) for i in range(2)]
        scV = SemC(new_sem("s_v"), 16)
        scXL = [SemC(new_sem(f"s_xl{i}"), 16) for i in range(S)]
        scXS = [SemC(new_sem(f"s_xs{i}"), 16) for i in range(S)]

        ctr = {"ps": 0, "wa": 0, "wb": 0, "tmp": 0, "sil": 0, "ads": 0}

        def nxt(key, n):
            v = ctr[key] % n
            ctr[key] += 1
            return v

        def sl(s):
            return slice(s * 512, (s + 1) * 512)

        def gs_ap(i, n, c=None):
            o = ((i * 3 + n) * 2 + 0) * 8
            return DER[:, o:o + 8] if c is None else DER[:, o + c:o + c + 1]

        def gt_ap(i, n, c=None):
            o = ((i * 3 + n) * 2 + 1) * 8
            return DER[:, o:o + 8] if c is None else DER[:, o + c:o + c + 1]

        FG_O = 2 * 3 * 2 * 8

        def fg_ap(c):
            return DER[:, FG_O + c:FG_O + c + 1]

        def sh_ap(i, n, c):
            o = i * 72 + n * 24 + 0 * 8 + c
            return MOD[:, o:o + 1]

        def E(st, meth, reads=(), writes=(), sc=None, inc=True, **kw):
            P.emit(st, meth, kw, reads=reads, writes=writes, sc=sc, inc=inc)

        E(sp, "dma_start", writes=[bV], sc=scV, out=V[:], in_=vecs[:, :])
        E(pool, "memset", writes=[bONES], ap=ONES[:], constant=1.0)
        E(pool, "memset", writes=[bEPS], ap=EPSV[:], constant=D * EPS)
        if first:
            E(pool, "memset", writes=[bCARP], ap=CARP[:], constant=0.0)
            E(pool, "memset", writes=[bCARU], ap=CARU[:], constant=0.0)
        else:
            scCI = SemC(new_sem("s_ci"), 16)
            E(sp, "dma_start", writes=[bCARP], sc=scCI, out=CARP[:], in_=cin[:, 0:DC * 16].rearrange("p (c k) -> p c k", k=16))
            E(sp, "dma_start", writes=[bCARU], sc=scCI, out=CARU[:], in_=cin[:, DC * 16:DC * 18].rearrange("p (c k) -> p c k", k=2))
        for g, w in enumerate(POOL_W):
            E(pool, "memset", writes=[bINVC], ap=INVC[:, g, :], constant=1.0 / w)
            for t in range(w - 1):
                E(pool, "memset", writes=[bINVC], ap=INVC[:, g, t:t + 1], constant=1.0 / (t + 1))
        E(act, "activation", reads=[bV], writes=[bCOND], out=COND[:], in_=V[:, V_C:V_C + 8], func=AF.Silu)

        def layer_setup(i):
            pb = nxt("ps", 8)
            for pc in range(36):
                a = nxt("ads", 2)
                E(sp, "dma_start", writes=[bADS[a]], sc=scADS[a], out=ADS[a][:], in_=adaw[i * 36 + pc, :, :])
                for mm in range(2):
                    col = pc * 2 + mm
                    for k in range(DC):
                        E(pe, "matmul", reads=[bADS[a], bCOND], writes=[bPS[pb]], inc=(k == DC - 1),
                          out=PSB[pb][:, col:col + 1],
                          lhsT=ADS[a][:, mm * 1024 + k * 128: mm * 1024 + (k + 1) * 128],
                          rhs=COND[:, k:k + 1], start=(k == 0), stop=(k == DC - 1))
            E(dve, "tensor_tensor", reads=[bPS[pb], bV], writes=[bMOD],
              out=MOD[:, i * 72:(i + 1) * 72], in0=PSB[pb][:, 0:72],
              in1=V[:, V_ADAB + i * 72: V_ADAB + (i + 1) * 72], op=ALU.add)
            for n in range(3):
                sc_o = i * 72 + n * 24 + 8
                gt_o = i * 72 + n * 24 + 16
                ng_o = V_NG + (i * 3 + n) * 8
                E(dve, "tensor_scalar", reads=[bMOD], writes=[bVT],
                  out=VT[:], in0=MOD[:, sc_o:sc_o + 8], scalar1=1.0, scalar2=SQRT_D,
                  op0=ALU.add, op1=ALU.mult)
                E(dve, "tensor_tensor", reads=[bVT, bV], writes=[bDER],
                  out=gs_ap(i, n), in0=VT[:], in1=V[:, ng_o:ng_o + 8], op=ALU.mult)
                if n != 1:
                    E(dve, "tensor_scalar", reads=[bMOD], writes=[bDER],
                      out=gt_ap(i, n), in0=MOD[:, gt_o:gt_o + 8], scalar1=0.5, scalar2=None, op0=ALU.mult)
                elif i % 2 == 0:
                    E(dve, "tensor_tensor", reads=[bMOD, bV], writes=[bDER],
                      out=gt_ap(i, n), in0=MOD[:, gt_o:gt_o + 8], in1=V[:, V_PS:V_PS + 8], op=ALU.mult)
                else:
                    E(dve, "tensor_copy", reads=[bMOD], writes=[bDER],
                      out=gt_ap(i, n), in_=MOD[:, gt_o:gt_o + 8])

        for i in range(n_layers):
            layer_setup(i)
        E(dve, "tensor_scalar", reads=[bV], writes=[bDER], out=DER[:, FG_O:FG_O + 8],
          in0=V[:, V_FG:V_FG + 8], scalar1=SQRT_D, scalar2=None, op0=ALU.mult)

        def rstd_for(s, out_ap, out_buf):
            q = s % 2
            for c in range(DC):
                E(act, "activation", reads=[bX[c][s]], writes=[bSQ[q]],
                  out=SQ[q][:, c, :], in_=X[:, c, sl(s)], func=AF.Square)
            pb = nxt("ps", 8)
            for c in range(DC):
                E(pe, "matmul", reads=[bONES, bSQ[q]], writes=[bPS[pb]], inc=(c == DC - 1),
                  out=PSB[pb][:], lhsT=ONES[:], rhs=SQ[q][:, c, :], start=(c == 0), stop=(c == DC - 1))
            E(act, "activation", reads=[bPS[pb], bEPS], writes=[out_buf],
              out=out_ap, in_=PSB[pb][:], func=AF.Ln, bias=EPSV[:], scale=1.0)
            E(act, "activation", reads=[out_buf], writes=[out_buf],
              out=out_ap, in_=out_ap, func=AF.Exp, scale=-0.5)

        def norm_mod(i, n):
            for s in range(S):
                q = s % 2
                rstd_for(s, RSTD[q][:], bRSTD[q])
                for c in range(DC):
                    t = nxt("tmp", NTMP)
                    E(dve, "scalar_tensor_tensor", reads=[bX[c][s], bDER, bRSTD[q]], writes=[bTMP[t]],
                      out=TMP[t][:], in0=X[:, c, sl(s)], scalar=gs_ap(i, n, c), in1=RSTD[q][:],
                      op0=ALU.mult, op1=ALU.mult)
                    E(act, "activation", reads=[bTMP[t], bMOD], writes=[bH[c][s]],
                      out=H[:, c, sl(s)], in_=TMP[t][:], func=AF.Identity, bias=sh_ap(i, n, c), scale=1.0)

        def resid_evac(pb, c, s, gate_ap):
            E(dve, "scalar_tensor_tensor", reads=[bPS[pb], bDER, bX[c][s]], writes=[bX[c][s]],
              out=X[:, c, sl(s)], in0=PSB[pb][:], scalar=gate_ap, in1=X[:, c, sl(s)],
              op0=ALU.mult, op1=ALU.add)

        def ffn(i, j, n):
            norm_mod(i, n)
            wi = i * 2 + j
            for (f0, nf) in fsplits:
                for fi in range(nf):
                    f = f0 + fi
                    a = nxt("wa", NWA)
                    E(pool, "dma_start", writes=[bWA[a]], sc=scWA[a],
                      out=WA[a][:, 0:2048], in_=win[wi * FC + f, :, :])
                    for s in range(S):
                        pa = nxt("ps", 8)
                        pbb = nxt("ps", 8)
                        for ab, pbk in ((0, pa), (1, pbb)):
                            for k in range(DC):
                                o = (k * 2 + ab) * 128
                                E(pe, "matmul", reads=[bWA[a], bH[k][s]], writes=[bPS[pbk]], inc=(k == DC - 1),
                                  out=PSB[pbk][:], lhsT=WA[a][:, o:o + 128], rhs=H[:, k, sl(s)],
                                  start=(k == 0), stop=(k == DC - 1))
                        t = nxt("sil", NSIL)
                        E(act, "activation", reads=[bPS[pa]], writes=[bSIL[t]],
                          out=SIL[t][:], in_=PSB[pa][:], func=AF.Silu)
                        E(dve, "tensor_tensor", reads=[bSIL[t], bPS[pbb]], writes=[bR[fi][s]],
                          out=R[:, fi, sl(s)], in0=SIL[t][:], in1=PSB[pbb][:], op=ALU.mult)
                for c in range(DC):
                    b = nxt("wb", NWB)
                    E(pool, "dma_start", writes=[bWB[b]], sc=scWB[b],
                      out=WB[b][:, 0:nf * 128], in_=wout[wi * DC + c, :, f0 * 128:(f0 + nf) * 128])
                    for s in range(S):
                        pb = nxt("ps", 8)
                        for fi in range(nf):
                            E(pe, "matmul", reads=[bWB[b], bR[fi][s]], writes=[bPS[pb]], inc=(fi == nf - 1),
                              out=PSB[pb][:], lhsT=WB[b][:, fi * 128:(fi + 1) * 128], rhs=R[:, fi, sl(s)],
                              start=(fi == 0), stop=(fi == nf - 1))
                        resid_evac(pb, c, s, gt_ap(i, n, c))

        def pool_mixer(i, tt):
            n = 1
            E_ = 16 + TT
            for s in range(S):
                rstd_for(s, RSA[:, sl(s)], bRSA[s])
            for qc in range(DC):
                g = qc // 2
                w = POOL_W[g]
                hp = qc % 2
                E(act, "copy", reads=[bCARP], writes=[bHP[hp]], out=HP[hp][:, 0:16], in_=CARP[:, qc, :])
                for s in range(S):
                    t = nxt("tmp", NTMP)
                    E(dve, "scalar_tensor_tensor", reads=[bX[qc][s], bDER, bRSA[s]], writes=[bTMP[t]],
                      out=TMP[t][:], in0=X[:, qc, sl(s)], scalar=gs_ap(i, n, qc), in1=RSA[:, sl(s)],
                      op0=ALU.mult, op1=ALU.mult)
                    E(act, "activation", reads=[bTMP[t], bMOD], writes=[bHP[hp]],
                      out=HP[hp][:, 16 + s * 512:16 + (s + 1) * 512], in_=TMP[t][:], func=AF.Identity,
                      bias=sh_ap(i, n, qc), scale=1.0)
                src, bsrc = HP[hp], bHP[hp]
                for lvl in range(g + 1):
                    d = 1 << lvl
                    lo = (1 << (lvl + 1)) - 1
                    dst, bdst = PL[lvl % 2], bPL[lvl % 2]
                    E(dve, "tensor_tensor", reads=[bsrc], writes=[bdst],
                      out=dst[:, lo:E_], in0=src[:, lo:E_], in1=src[:, lo - d:E_ - d], op=ALU.add)
                    src, bsrc = dst, bdst
                E(dve, "scalar_tensor_tensor", reads=[bsrc, bHP[hp]], writes=[bR[qc][s] for s in range(S)],
                  out=R[:, qc, :], in0=src[:, 16:E_], scalar=1.0 / w, in1=HP[hp][:, 16:E_],
                  op0=ALU.mult, op1=ALU.subtract)
                if tt == 0 and first:
                    E(dve, "tensor_tensor", reads=[bsrc, bINVC], writes=[bFIX],
                      out=FIX[:], in0=src[:, 16:32], in1=INVC[:, g, :], op=ALU.mult)
                    E(dve, "tensor_tensor", reads=[bFIX, bHP[hp]], writes=[bR[qc][0]],
                      out=R[:, qc, 0:16], in0=FIX[:], in1=HP[hp][:, 16:32], op=ALU.subtract)
                E(act, "copy", reads=[bHP[hp]], writes=[bCARP], out=CARP[:, qc, :], in_=HP[hp][:, TT:TT + 16])
            for g in range(4):
                b = nxt("wb", NWB)
                E(pool, "dma_start", writes=[bWB[b]], sc=scWB[b], out=WB[b][:, 0:512], in_=poolw[g, :, :])
                for m in range(2):
                    c = 2 * g + m
                    for s in range(S):
                        pb = nxt("ps", 8)
                        for kq in range(2):
                            o = kq * 256 + m * 128
                            E(pe, "matmul", reads=[bWB[b], bR[2 * g + kq][s]], writes=[bPS[pb]], inc=(kq == 1),
                              out=PSB[pb][:], lhsT=WB[b][:, o:o + 128], rhs=R[:, 2 * g + kq, sl(s)],
                              start=(kq == 0), stop=(kq == 1))
                        resid_evac(pb, c, s, gt_ap(i, n, c))

        def conv_mixer(i, tt):
            n = 1
            norm_mod(i, n)
            for j in range(DC):
                a = nxt("wa", NWA)
                E(pool, "dma_start", writes=[bWA[a]], sc=scWA[a], out=WA[a][:, 0:3072], in_=cwin[j, :, :])
                u = j % 2
                E(act, "copy", reads=[bCARU], writes=[bUE[u]], out=UE[u][:, 0:2], in_=CARU[:, j, :])

                def cw(kk):
                    return V[:, V_CW + kk * 8 + j: V_CW + kk * 8 + j + 1]

                for s in range(S):
                    pbs = [nxt("ps", 8) for _ in range(3)]
                    for r in range(3):
                        for k in range(DC):
                            o = (k * 3 + r) * 128
                            E(pe, "matmul", reads=[bWA[a], bH[k][s]], writes=[bPS[pbs[r]]], inc=(k == DC - 1),
                              out=PSB[pbs[r]][:], lhsT=WA[a][:, o:o + 128], rhs=H[:, k, sl(s)],
                              start=(k == 0), stop=(k == DC - 1))
                    pB, pC, pV = pbs
                    t1 = nxt("sil", NSIL)
                    t2 = nxt("tmp", NTMP)
                    lo = 2 + s * 512
                    E(act, "copy", reads=[bPS[pC]], writes=[bSIL[t1]], out=SIL[t1][:], in_=PSB[pC][:])
                    E(dve, "tensor_tensor", reads=[bSIL[t1], bPS[pV]], writes=[bUE[u]],
                      out=UE[u][:, lo:lo + 512], in0=SIL[t1][:], in1=PSB[pV][:], op=ALU.mult)
                    E(act, "activation", reads=[bUE[u], bV], writes=[bTMP[t2]],
                      out=TMP[t2][:], in_=UE[u][:, lo:lo + 512], func=AF.Identity, scale=cw(2))
                    for kk in (1, 0):
                        sh = 2 - kk
                        E(dve, "scalar_tensor_tensor", reads=[bUE[u], bV, bTMP[t2]], writes=[bTMP[t2]],
                          out=TMP[t2][:], in0=UE[u][:, lo - sh:lo - sh + 512], scalar=cw(kk), in1=TMP[t2][:],
                          op0=ALU.mult, op1=ALU.add)
                    E(dve, "tensor_tensor", reads=[bTMP[t2], bPS[pB]], writes=[bR[j][s]],
                      out=R[:, j, sl(s)], in0=TMP[t2][:], in1=PSB[pB][:], op=ALU.mult)
                E(act, "copy", reads=[bUE[u]], writes=[bCARU], out=CARU[:, j, :], in_=UE[u][:, TT:TT + 2])
            for c in range(DC):
                b = nxt("wb", NWB)
                E(pool, "dma_start", writes=[bWB[b]], sc=scWB[b], out=WB[b][:, 0:1024], in_=cwout[c, :, :])
                for s in range(S):
                    pb = nxt("ps", 8)
                    for j in range(DC):
                        E(pe, "matmul", reads=[bWB[b], bR[j][s]], writes=[bPS[pb]], inc=(j == DC - 1),
                          out=PSB[pb][:], lhsT=WB[b][:, j * 128:(j + 1) * 128], rhs=R[:, j, sl(s)],
                          start=(j == 0), stop=(j == DC - 1))
                    resid_evac(pb, c, s, gt_ap(i, n, c))

        def final_store(tt):
            for s in range(S):
                q = s % 2
                if do_final:
                    rstd_for(s, RSTD[q][:], bRSTD[q])
                    for c in range(DC):
                        E(dve, "scalar_tensor_tensor", reads=[bX[c][s], bDER, bRSTD[q]], writes=[bX[c][s]],
                          out=X[:, c, sl(s)], in0=X[:, c, sl(s)], scalar=fg_ap(c), in1=RSTD[q][:],
                          op0=ALU.mult, op1=ALU.mult)
                t0 = tt * TT + s * 512
                E(sp, "dma_start", reads=[bX[c][s] for c in range(DC)], sc=scXS[s],
                  out=outT3[:, :, t0:t0 + 512], in_=X[:, :, sl(s)])

        for tt in range(NT):
            for s in range(S):
                t0 = tt * TT + s * 512
                E(sp, "dma_start", writes=[bX[c][s] for c in range(DC)], sc=scXL[s],
                  out=X[:, :, sl(s)], in_=xT3[:, :, t0:t0 + 512])
            for i in range(n_layers):
                ffn(i, 0, 0)
                if i % 2 == 0:
                    pool_mixer(i, tt)
                else:
                    conv_mixer(i, tt)
                ffn(i, 1, 2)
            final_store(tt)

        if carry_io:
            scCO = SemC(new_sem("s_co"), 16)
            E(sp, "dma_start", reads=[bCARP], sc=scCO, out=cout[:, 0:DC * 16].rearrange("p (c k) -> p c k", k=16), in_=CARP[:])
            sp.items.append(("wait", scCO.sem, scCO.count * 16))
            E(sp, "dma_start", reads=[bCARU], sc=scCO, out=cout[:, DC * 16:DC * 18].rearrange("p (c k) -> p c k", k=2), in_=CARU[:])
            sp.items.append(("wait", scCO.sem, scCO.count * 16))
        for s in range(S):
            sp.items.append(("wait", scXS[s].sem, scXS[s].count * 16))

        with nc.Block() as block:
            block.tensor(lambda e: replay(pe.items, e))
            block.scalar(lambda e: replay(act.items, e))
            block.vector(lambda e: replay(dve.items, e))
            block.gpsimd(lambda e: replay(pool.items, e))
            block.sync(lambda e: replay(sp.items, e))
    return nc


def _cols(v):
    v = np.asarray(v, dtype=np.float32).reshape(-1, 128)
    return np.ascontiguousarray(v.T)


def prep_weights(ada_w, ada_b, norm_g, ffn_w_in, ffn_w_out, pool_w, pool_scale, conv_w_in, conv_w,
                 conv_w_out, final_g):
    f32 = np.float32
    vec_common = np.concatenate([
        _cols(np.asarray(ada_b, f32).reshape(-1)),
        _cols(np.asarray(norm_g, f32).reshape(-1)),
        _cols(np.asarray(pool_scale, f32).reshape(-1)),
        _cols(np.asarray(conv_w, f32).reshape(-1)),
        _cols(np.asarray(final_g, f32).reshape(-1)),
    ], axis=1)
    aw = np.asarray(ada_w, f32).reshape(2, DC, 128, 36, 2, 128)
    adaw = np.ascontiguousarray(aw.transpose(0, 3, 2, 4, 1, 5)).reshape(72, 128, 2048)
    wi_ = np.asarray(ffn_w_in, f32).reshape(4, DC, 128, 2, FC, 128)
    win = np.ascontiguousarray(wi_.transpose(0, 4, 2, 1, 3, 5)).reshape(4 * FC, 128, 2048)
    wo_ = np.asarray(ffn_w_out, f32).reshape(4, FC, 128, DC, 128)
    wout = np.ascontiguousarray(wo_.transpose(0, 3, 2, 1, 4)).reshape(4 * DC, 128, FC * 128)
    pw_ = np.asarray(pool_w, f32).reshape(4, 2, 128, 256)
    poolw = np.ascontiguousarray(pw_.transpose(0, 2, 1, 3)).reshape(4, 128, 512)
    ci_ = np.asarray(conv_w_in, f32).reshape(DC, 128, 3, DC, 128)
    cwin = np.ascontiguousarray(ci_.transpose(3, 1, 0, 2, 4)).reshape(DC, 128, 3072)
    co_ = np.asarray(conv_w_out, f32).reshape(DC, 128, DC, 128)
    cwout = np.ascontiguousarray(co_.transpose(2, 1, 0, 3)).reshape(DC, 128, 1024)
    return vec_common, dict(adaw=adaw, win=win, wout=wout, poolw=poolw, cwin=cwin, cwout=cwout)


def run(inputs, T=SEQ, TT=1024, FS=2, n_layers=2, do_final=True, trace=False, n_seg=1):
    x = np.asarray(inputs["x"], np.float32)
    c = np.asarray(inputs["c"], np.float32)
    vec_common, wd = prep_weights(*[inputs[k] for k in (
        "ada_w", "ada_b", "norm_g", "ffn_w_in", "ffn_w_out", "pool_w", "pool_scale",
        "conv_w_in", "conv_w", "conv_w_out", "final_g")])
    Ts = T // n_seg
    outs = [[] for _ in range(NB)]
    carry = None
    res = None
    for g in range(n_seg):
        nc = build_nc(Ts, TT, FS, n_layers, do_final, first=(g == 0), carry_io=(n_seg > 1))
        in_maps = []
        for b in range(NB):
            m = dict(wd)
            m["xT"] = np.ascontiguousarray(x[b, g * Ts:(g + 1) * Ts, :].T)
            m["vecs"] = np.ascontiguousarray(np.concatenate([_cols(c[b]), vec_common], axis=1))
            if n_seg > 1:
                m["cin"] = carry[b] if carry is not None else np.zeros((128, DC * 18), np.float32)
            in_maps.append(m)
        res = run_bass_kernel_spmd(nc, in_maps, core_ids=list(range(NB)), trace=trace)
        for b, r in enumerate(res.results):
            outs[b].append(np.ascontiguousarray(r["outT"].T))
        if n_seg > 1:
            carry = [np.ascontiguousarray(r["cout"]) for r in res.results]
    out = np.stack([np.concatenate(o, axis=0) for o in outs], axis=0)
    return out.astype(np.float32, copy=False), res


def kernel(x, c, ada_w, ada_b, norm_g, ffn_w_in, ffn_w_out, pool_w, pool_scale,
           conv_w_in, conv_w, conv_w_out, final_g):
    out, _ = run(dict(x=x, c=c, ada_w=ada_w, ada_b=ada_b, norm_g=norm_g, ffn_w_in=ffn_w_in,
                      ffn_w_out=ffn_w_out, pool_w=pool_w, pool_scale=pool_scale, conv_w_in=conv_w_in,
                      conv_w=conv_w, conv_w_out=conv_w_out, final_g=final_g), n_seg=2)
    return out
```

```python
import contextlib
import numpy as np
import concourse.bass as bass
import concourse.mybir as mybir
from concourse.bass_utils import run_bass_kernel_spmd

F32 = mybir.dt.float32
BF16 = mybir.dt.bfloat16
AF = mybir.ActivationFunctionType
ALU = mybir.AluOpType

D = 1024
DC = D // 128
DFF = 2816
FC = DFF // 128
SEQ = 8192
NB = 8
EPS = 1e-6
POOL_W = (2, 4, 8, 16)
SQRT_D = float(np.sqrt(np.float32(D)))

V_C = 0
V_ADAB = V_C + 8
V_NG = V_ADAB + 144
V_PS = V_NG + 48
V_CW = V_PS + 8
V_FG = V_CW + 24
NV = V_FG + 8


class SemC:
    def __init__(self, sem, unit):
        self.sem = sem
        self.unit = unit
        self.count = 0


class Buf:
    __slots__ = ("w", "r")

    def __init__(self):
        self.w = None
        self.r = {}


class Stream:
    def __init__(self, name, sc, is_pe=False):
        self.name = name
        self.sc = sc
        self.is_pe = is_pe
        self.items = []
        self.seen = {}


class Plan:
    def emit(self, st, meth, kw, reads=(), writes=(), sc=None, inc=True):
        fn = (meth, kw)
        deps = {}

        def need(s, c):
            if s is st.sc and st.is_pe:
                return
            if deps.get(s, 0) < c:
                deps[s] = c

        for b in reads:
            if b.w is not None:
                need(*b.w)
        for b in writes:
            if b.w is not None:
                need(*b.w)
            for s, c in b.r.items():
                need(s, c)
        for s, c in deps.items():
            if st.seen.get(s, 0) >= c:
                continue
            st.items.append(("wait", s.sem, c * s.unit))
            st.seen[s] = c
        tgt = sc if sc is not None else st.sc
        if inc:
            tgt.count += 1
            st.items.append(("op", fn, tgt.sem, tgt.unit))
            stamp = (tgt, tgt.count)
        else:
            st.items.append(("op", fn, None, 0))
            stamp = (tgt, tgt.count + 1)
        for b in reads:
            if b.r.get(stamp[0], 0) < stamp[1]:
                b.r[stamp[0]] = stamp[1]
        for b in writes:
            b.w = stamp
            b.r = {}


def replay(items, eng):
    for it in items:
        if it[0] == "wait":
            eng.wait_ge(it[1], it[2])
        else:
            ins = getattr(eng, it[1][0])(**it[1][1])
            if it[2] is not None:
                ins.then_inc(it[2], it[3])


def build_nc(T, TT, FS=2, n_layers=2, do_final=True, first=True, carry_io=False):
    S = TT // 512
    NT = T // TT
    assert T % TT == 0 and TT % 512 == 0
    base, rem = divmod(FC, FS)
    fsplits = []
    f0 = 0
    for q in range(FS):
        n = base + (1 if q < rem else 0)
        fsplits.append((f0, n))
        f0 += n
    NR = max(max(n for _, n in fsplits), DC)

    nc = bass.Bass("TRN2", target_bir_lowering=False)
    xT = nc.dram_tensor("xT", [D, T], F32, kind="ExternalInput").ap()
    vecs = nc.dram_tensor("vecs", [128, NV], F32, kind="ExternalInput").ap()
    adaw = nc.dram_tensor("adaw", [2 * 72, 128, 1024], F32, kind="ExternalInput").ap()
    cbx = nc.dram_tensor("cbx", [128, 1024], F32, kind="ExternalInput").ap()
    win = nc.dram_tensor("win", [4 * FC, 128, 2048], F32, kind="ExternalInput").ap()
    wout = nc.dram_tensor("wout", [4 * DC, 128, FC * 128], F32, kind="ExternalInput").ap()
    poolw = nc.dram_tensor("poolw", [4, 128, 512], F32, kind="ExternalInput").ap()
    cwin = nc.dram_tensor("cwin", [DC, 128, 3072], F32, kind="ExternalInput").ap()
    cwout = nc.dram_tensor("cwout", [DC, 128, 1024], F32, kind="ExternalInput").ap()
    outT = nc.dram_tensor("outT", [D, T], F32, kind="ExternalOutput").ap()
    if carry_io:
        cin = nc.dram_tensor("cin", [128, DC * 18], F32, kind="ExternalInput").ap()
        cout = nc.dram_tensor("cout", [128, DC * 18], F32, kind="ExternalOutput").ap()
    xT3 = xT.rearrange("(c p) t -> p c t", p=128)
    outT3 = outT.rearrange("(c p) t -> p c t", p=128)

    es = contextlib.ExitStack()
    with es:
        def sb(name, shape, dt):
            return es.enter_context(nc.sbuf_tensor(name, shape, dt))

        def new_sem(name):
            return es.enter_context(nc.semaphore(name))

        X = sb("X", [128, DC, TT], F32)
        H = sb("H", [128, DC, TT], BF16)
        R = sb("R", [128, NR, TT], BF16)
        NWA, NWB = 3, 3
        WA = [sb(f"WA{i}", [128, 3072], BF16) for i in range(NWA)]
        WB = [sb(f"WB{i}", [128, 11 * 128], BF16) for i in range(NWB)]
        NADS = 4
        ADS = [sb(f"ADS{i}", [128, 1024], F32) for i in range(NADS)]
        CB = sb("CB", [128, 1024], F32)
        SCR = sb("SCR", [128, 1024], F32)
        MODR = sb("MODR", [128, 2 * 72], F32)
        V = sb("V", [128, NV], F32)
        ONES = sb("ONES", [128, 128], BF16)
        COND = sb("COND", [128, 8], F32)
        MOD = sb("MOD", [128, 2 * 72], F32)
        DER = sb("DER", [128, 2 * 3 * 2 * 8 + 8], F32)
        VT = sb("VT", [128, 8], F32)
        EPSV = sb("EPSV", [128, 1], F32)
        SQ = [sb(f"SQ{i}", [128, DC, 512], BF16) for i in range(S)]
        RSTD = [sb(f"RSTD{i}", [128, 512], F32) for i in range(2)]
        NTMP = 3
        TMP = [sb(f"TMP{i}", [128, 512], F32) for i in range(NTMP)]
        NSIL = 3
        SIL = [sb(f"SIL{i}", [128, 512], F32) for i in range(NSIL)]
        RSA = sb("RSA", [128, TT], F32)
        HP = [sb(f"HP{i}", [128, 16 + TT], F32) for i in range(2)]
        PL = [sb(f"PL{i}", [128, 16 + TT], F32) for i in range(2)]
        CARP = sb("CARP", [128, DC, 16], F32)
        INVC = sb("INVC", [128, 4, 16], F32)
        FIX = sb("FIX", [128, 16], F32)
        UE = [sb(f"UE{i}", [128, 2 + TT], F32) for i in range(2)]
        CARU = sb("CARU", [128, DC, 2], F32)
        PSB = [es.enter_context(nc.psum_tensor(f"PS{i}", [128, 512], F32)) for i in range(8)]

        pe = Stream("pe", SemC(new_sem("s_pe"), 1), is_pe=True)
        act = Stream("act", SemC(new_sem("s_act"), 1))
        dve = Stream("dve", SemC(new_sem("s_dve"), 1))
        pool = Stream("pool", SemC(new_sem("s_pool"), 1))
        sp = Stream("sp", SemC(new_sem("s_sp"), 1))
        P = Plan()

        bX = [[Buf() for _ in range(S)] for _ in range(DC)]
        bH = [[Buf() for _ in range(S)] for _ in range(DC)]
        bR = [[Buf() for _ in range(S)] for _ in range(NR)]
        bWA = [Buf() for _ in range(NWA)]
        bWB = [Buf() for _ in range(NWB)]
        bADS = [Buf() for _ in range(NADS)]
        bCB, bSCR = Buf(), Buf()
        bMODR = [[Buf() for _ in range(3)] for _ in range(2)]
        bV, bONES, bCOND, bMOD, bDER, bVT = Buf(), Buf(), Buf(), Buf(), Buf(), Buf()
        bEPS = Buf()
        bSQ = [[Buf() for _ in range(DC)] for _ in range(S)]
        bRSTD = [Buf() for _ in range(2)]
        bTMP = [Buf() for _ in range(NTMP)]
        bSIL = [Buf() for _ in range(NSIL)]
        bRSA = [Buf() for _ in range(S)]
        bHP = [Buf() for _ in range(2)]
        bPL = [Buf() for _ in range(2)]
        bCARP, bINVC, bFIX = Buf(), Buf(), Buf()
        bUE = [Buf() for _ in range(2)]
        bCARU = Buf()
        bPS = [Buf() for _ in range(8)]

        scWA = [SemC(new_sem(f"s_wa{i}"), 16) for i in range(NWA)]
        scWB = [SemC(new_sem(f"s_wb{i}"), 16) for i in range(NWB)]
        scADS = [SemC(new_sem(f"s_ads{i}"), 16) for i in range(NADS)]
        scCB = SemC(new_sem("s_cb"), 16)
        scV = SemC(new_sem("s_v"), 16)
        scXL = [SemC(new_sem(f"s_xl{i}"), 16) for i in range(S)]
        scXS = [SemC(new_sem(f"s_xs{i}"), 16) for i in range(S)]

        ctr = {"ps": 0, "wa": 0, "wb": 0, "tmp": 0, "sil": 0, "ads": 0}

        def nxt(key, n):
            v = ctr[key] % n
            ctr[key] += 1
            return v

        def sl(s):
            return slice(s * 512, (s + 1) * 512)

        def gs_ap(i, n, c=None):
            o = ((i * 3 + n) * 2 + 0) * 8
            return DER[:, o:o + 8] if c is None else DER[:, o + c:o + c + 1]

        def gt_ap(i, n, c=None):
            o = ((i * 3 + n) * 2 + 1) * 8
            return DER[:, o:o + 8] if c is None else DER[:, o + c:o + c + 1]

        FG_O = 2 * 3 * 2 * 8

        def fg_ap(c):
            return DER[:, FG_O + c:FG_O + c + 1]

        def sh_ap(i, n, c):
            o = i * 72 + n * 24 + 0 * 8 + c
            return MOD[:, o:o + 1]

        def E(st, meth, reads=(), writes=(), sc=None, inc=True, **kw):
            P.emit(st, meth, kw, reads=reads, writes=writes, sc=sc, inc=inc)

        E(sp, "dma_start", writes=[bV], sc=scV, out=V[:], in_=vecs[:, :])
        E(pool, "memset", writes=[bONES], ap=ONES[:], constant=1.0)
        E(pool, "memset", writes=[bEPS], ap=EPSV[:], constant=D * EPS)
        if first:
            E(pool, "memset", writes=[bCARP], ap=CARP[:], constant=0.0)
            E(pool, "memset", writes=[bCARU], ap=CARU[:], constant=0.0)
        else:
            scCI = SemC(new_sem("s_ci"), 16)
            E(sp, "dma_start", writes=[bCARP], sc=scCI, out=CARP[:], in_=cin[:, 0:DC * 16].rearrange("p (c k) -> p c k", k=16))
            E(sp, "dma_start", writes=[bCARU], sc=scCI, out=CARU[:], in_=cin[:, DC * 16:DC * 18].rearrange("p (c k) -> p c k", k=2))
        for g, w in enumerate(POOL_W):
            E(pool, "memset", writes=[bINVC], ap=INVC[:, g, :], constant=1.0 / w)
            for t in range(w - 1):
                E(pool, "memset", writes=[bINVC], ap=INVC[:, g, t:t + 1], constant=1.0 / (t + 1))
        E(sp, "dma_start", writes=[bCB], sc=scCB, out=CB[:], in_=cbx[:, :])
        E(act, "activation", reads=[bCB], writes=[bCB], out=CB[:], in_=CB[:], func=AF.Silu)

        def mod_task(i, m):
            a = nxt("ads", NADS)
            n = m // 24
            E(sp, "dma_start", writes=[bADS[a]], sc=scADS[a], out=ADS[a][:], in_=adaw[i * 72 + m, :, :])
            E(dve, "scalar_tensor_tensor", reads=[bADS[a], bCB], writes=[bSCR, bMODR[i][n]],
              out=SCR[:], in0=ADS[a][:], scalar=1.0, in1=CB[:], op0=ALU.mult, op1=ALU.mult,
              accum_out=MODR[:, i * 72 + m:i * 72 + m + 1])

        def sub_setup(i, n):
            o = i * 72 + n * 24
            E(dve, "tensor_tensor", reads=[bMODR[i][n], bV], writes=[bMOD],
              out=MOD[:, o:o + 24], in0=MODR[:, o:o + 24], in1=V[:, V_ADAB + o:V_ADAB + o + 24], op=ALU.add)
            sc_o, gt_o = o + 8, o + 16
            ng_o = V_NG + (i * 3 + n) * 8
            E(dve, "tensor_scalar", reads=[bMOD], writes=[bVT],
              out=VT[:], in0=MOD[:, sc_o:sc_o + 8], scalar1=1.0, scalar2=SQRT_D, op0=ALU.add, op1=ALU.mult)
            E(dve, "tensor_tensor", reads=[bVT, bV], writes=[bDER],
              out=gs_ap(i, n), in0=VT[:], in1=V[:, ng_o:ng_o + 8], op=ALU.mult)
            if n != 1:
                E(dve, "tensor_scalar", reads=[bMOD], writes=[bDER],
                  out=gt_ap(i, n), in0=MOD[:, gt_o:gt_o + 8], scalar1=0.5, scalar2=None, op0=ALU.mult)
            elif i % 2 == 0:
                E(dve, "tensor_tensor", reads=[bMOD, bV], writes=[bDER],
                  out=gt_ap(i, n), in0=MOD[:, gt_o:gt_o + 8], in1=V[:, V_PS:V_PS + 8], op=ALU.mult)
            else:
                E(dve, "tensor_copy", reads=[bMOD], writes=[bDER], out=gt_ap(i, n), in_=MOD[:, gt_o:gt_o + 8])

        bg = [(i, m) for i in range(n_layers) for m in range(72)]

        def bg_pop(k):
            for _ in range(k):
                if not bg:
                    return
                i, m = bg.pop(0)
                mod_task(i, m)
                if m % 24 == 23:
                    sub_setup(i, m // 24)

        bg_pop(24)
        E(dve, "tensor_scalar", reads=[bV], writes=[bDER], out=DER[:, FG_O:FG_O + 8],
          in0=V[:, V_FG:V_FG + 8], scalar1=SQRT_D, scalar2=None, op0=ALU.mult)

        NROT = 8 - S

        def ssb(s):
            return NROT + s

        steps = []

        def step(fn):
            steps.append((None, None, 0, fn))

        def wstep(pool_id, src_ap, ncols, fn):
            steps.append((pool_id, src_ap, ncols, fn))

        sq_ready = [False] * S
        pending_ss = []

        def flush_ss():
            while pending_ss:
                c, s = pending_ss.pop(0)
                E(pe, "matmul", reads=[bONES, bSQ[s][c]], writes=[bPS[ssb(s)]], inc=(c == DC - 1),
                  out=PSB[ssb(s)][:], lhsT=ONES[:], rhs=SQ[s][:, c, :], start=(c == 0), stop=(c == DC - 1))
                if c == DC - 1:
                    sq_ready[s] = True

        def square(c, s):
            E(act, "activation", reads=[bX[c][s]], writes=[bSQ[s][c]],
              out=SQ[s][:, c, :], in_=X[:, c, sl(s)], func=AF.Square)
            pending_ss.append((c, s))

        def rstd_for(s, out_ap, out_buf):
            if not sq_ready[s]:
                for c in range(DC):
                    square(c, s)
                flush_ss()
            sq_ready[s] = False
            pb = ssb(s)
            E(act, "activation", reads=[bPS[pb], bEPS], writes=[out_buf],
              out=out_ap, in_=PSB[pb][:], func=AF.Ln, bias=EPSV[:], scale=1.0)
            E(act, "activation", reads=[out_buf], writes=[out_buf],
              out=out_ap, in_=out_ap, func=AF.Exp, scale=-0.5)

        def ensure_sub(i, n):
            while bg and (bg[0][0], bg[0][1] // 24) <= (i, n):
                bg_pop(1)

        def norm_mod(i, n):
            ensure_sub(i, n)
            for s in range(S):
                q = s % 2
                rstd_for(s, RSTD[q][:], bRSTD[q])
                for c in range(DC):
                    t = nxt("tmp", NTMP)
                    E(dve, "scalar_tensor_tensor", reads=[bX[c][s], bDER, bRSTD[q]], writes=[bTMP[t]],
                      out=TMP[t][:], in0=X[:, c, sl(s)], scalar=gs_ap(i, n, c), in1=RSTD[q][:],
                      op0=ALU.mult, op1=ALU.mult)
                    E(act, "activation", reads=[bTMP[t], bMOD], writes=[bH[c][s]],
                      out=H[:, c, sl(s)], in_=TMP[t][:], func=AF.Identity, bias=sh_ap(i, n, c), scale=1.0)

        def resid_evac(pb, c, s, gate_ap, early):
            if early:
                flush_ss()
            E(dve, "scalar_tensor_tensor", reads=[bPS[pb], bDER, bX[c][s]], writes=[bX[c][s]],
              out=X[:, c, sl(s)], in0=PSB[pb][:], scalar=gate_ap, in1=X[:, c, sl(s)],
              op0=ALU.mult, op1=ALU.add)
            if early:
                square(c, s)

        def ffn(i, j, n):
            step(lambda: norm_mod(i, n))
            wi = i * 2 + j
            for qi, (f0, nf) in enumerate(fsplits):
                last = (qi == len(fsplits) - 1)
                for fi in range(nf):
                    def phase_a(W, bW, fi=fi):
                        for s in range(S):
                            pa = nxt("ps", NROT)
                            pbb = nxt("ps", NROT)
                            for ab, pbk in ((0, pa), (1, pbb)):
                                for k in range(DC):
                                    o = (k * 2 + ab) * 128
                                    E(pe, "matmul", reads=[bW, bH[k][s]], writes=[bPS[pbk]], inc=(k == DC - 1),
                                      out=PSB[pbk][:], lhsT=W[:, o:o + 128], rhs=H[:, k, sl(s)],
                                      start=(k == 0), stop=(k == DC - 1))
                            t = nxt("sil", NSIL)
                            E(act, "activation", reads=[bPS[pa]], writes=[bSIL[t]],
                              out=SIL[t][:], in_=PSB[pa][:], func=AF.Silu)
                            E(dve, "tensor_tensor", reads=[bSIL[t], bPS[pbb]], writes=[bR[fi][s]],
                              out=R[:, fi, sl(s)], in0=SIL[t][:], in1=PSB[pbb][:], op=ALU.mult)
                    wstep("A", win[wi * FC + f0 + fi, :, :], 2048, phase_a)
                for c in range(DC):
                    def phase_b(W, bW, c=c, nf=nf, last=last):
                        for s in range(S):
                            pb = nxt("ps", NROT)
                            for fi in range(nf):
                                E(pe, "matmul", reads=[bW, bR[fi][s]], writes=[bPS[pb]], inc=(fi == nf - 1),
                                  out=PSB[pb][:], lhsT=W[:, fi * 128:(fi + 1) * 128], rhs=R[:, fi, sl(s)],
                                  start=(fi == 0), stop=(fi == nf - 1))
                            resid_evac(pb, c, s, gt_ap(i, n, c), last)
                    wstep("B", wout[wi * DC + c, :, f0 * 128:(f0 + nf) * 128], nf * 128, phase_b)
            step(flush_ss)

        def pool_mixer(i, tt):
            n = 1
            E_ = 16 + TT

            def elementwise():
                ensure_sub(i, n)
                for s in range(S):
                    rstd_for(s, RSA[:, sl(s)], bRSA[s])
                for qc in range(DC):
                    g = qc // 2
                    w = POOL_W[g]
                    hp = qc % 2
                    E(act, "copy", reads=[bCARP], writes=[bHP[hp]], out=HP[hp][:, 0:16], in_=CARP[:, qc, :])
                    for s in range(S):
                        t = nxt("tmp", NTMP)
                        E(dve, "scalar_tensor_tensor", reads=[bX[qc][s], bDER, bRSA[s]], writes=[bTMP[t]],
                          out=TMP[t][:], in0=X[:, qc, sl(s)], scalar=gs_ap(i, n, qc), in1=RSA[:, sl(s)],
                          op0=ALU.mult, op1=ALU.mult)
                        E(act, "activation", reads=[bTMP[t], bMOD], writes=[bHP[hp]],
                          out=HP[hp][:, 16 + s * 512:16 + (s + 1) * 512], in_=TMP[t][:], func=AF.Identity,
                          bias=sh_ap(i, n, qc), scale=1.0)
                    src_, bsrc = HP[hp], bHP[hp]
                    for lvl in range(g + 1):
                        d = 1 << lvl
                        lo = (1 << (lvl + 1)) - 1
                        dst, bdst = PL[lvl % 2], bPL[lvl % 2]
                        E(dve, "tensor_tensor", reads=[bsrc], writes=[bdst],
                          out=dst[:, lo:E_], in0=src_[:, lo:E_], in1=src_[:, lo - d:E_ - d], op=ALU.add)
                        src_, bsrc = dst, bdst
                    E(dve, "scalar_tensor_tensor", reads=[bsrc, bHP[hp]], writes=[bR[qc][s] for s in range(S)],
                      out=R[:, qc, :], in0=src_[:, 16:E_], scalar=1.0 / w, in1=HP[hp][:, 16:E_],
                      op0=ALU.mult, op1=ALU.subtract)
                    if tt == 0 and first:
                        E(dve, "tensor_tensor", reads=[bsrc, bINVC], writes=[bFIX],
                          out=FIX[:], in0=src_[:, 16:32], in1=INVC[:, g, :], op=ALU.mult)
                        E(dve, "tensor_tensor", reads=[bFIX, bHP[hp]], writes=[bR[qc][0]],
                          out=R[:, qc, 0:16], in0=FIX[:], in1=HP[hp][:, 16:32], op=ALU.subtract)
                    E(act, "copy", reads=[bHP[hp]], writes=[bCARP], out=CARP[:, qc, :], in_=HP[hp][:, TT:TT + 16])
            step(elementwise)
            for g in range(4):
                def mm(W, bW, g=g):
                    for m in range(2):
                        c = 2 * g + m
                        for s in range(S):
                            pb = nxt("ps", NROT)
                            for kq in range(2):
                                o = kq * 256 + m * 128
                                E(pe, "matmul", reads=[bW, bR[2 * g + kq][s]], writes=[bPS[pb]], inc=(kq == 1),
                                  out=PSB[pb][:], lhsT=W[:, o:o + 128], rhs=R[:, 2 * g + kq, sl(s)],
                                  start=(kq == 0), stop=(kq == 1))
                            resid_evac(pb, c, s, gt_ap(i, n, c), True)
                wstep("B", poolw[g, :, :], 512, mm)
            step(flush_ss)

        def conv_mixer(i, tt):
            n = 1
            step(lambda: norm_mod(i, n))
            for j in range(DC):
                def cj(W, bW, j=j):
                    u = j % 2
                    E(act, "copy", reads=[bCARU], writes=[bUE[u]], out=UE[u][:, 0:2], in_=CARU[:, j, :])

                    def cw(kk):
                        return V[:, V_CW + kk * 8 + j: V_CW + kk * 8 + j + 1]

                    for s in range(S):
                        pbs = [nxt("ps", NROT) for _ in range(3)]
                        for r in range(3):
                            for k in range(DC):
                                o = (k * 3 + r) * 128
                                E(pe, "matmul", reads=[bW, bH[k][s]], writes=[bPS[pbs[r]]], inc=(k == DC - 1),
                                  out=PSB[pbs[r]][:], lhsT=W[:, o:o + 128], rhs=H[:, k, sl(s)],
                                  start=(k == 0), stop=(k == DC - 1))
                        pB, pC, pV = pbs
                        t1 = nxt("sil", NSIL)
                        t2 = nxt("tmp", NTMP)
                        lo = 2 + s * 512
                        E(act, "copy", reads=[bPS[pC]], writes=[bSIL[t1]], out=SIL[t1][:], in_=PSB[pC][:])
                        E(dve, "tensor_tensor", reads=[bSIL[t1], bPS[pV]], writes=[bUE[u]],
                          out=UE[u][:, lo:lo + 512], in0=SIL[t1][:], in1=PSB[pV][:], op=ALU.mult)
                        E(act, "activation", reads=[bUE[u], bV], writes=[bTMP[t2]],
                          out=TMP[t2][:], in_=UE[u][:, lo:lo + 512], func=AF.Identity, scale=cw(2))
                        for kk in (1, 0):
                            sh = 2 - kk
                            E(dve, "scalar_tensor_tensor", reads=[bUE[u], bV, bTMP[t2]], writes=[bTMP[t2]],
                              out=TMP[t2][:], in0=UE[u][:, lo - sh:lo - sh + 512], scalar=cw(kk), in1=TMP[t2][:],
                              op0=ALU.mult, op1=ALU.add)
                        E(dve, "tensor_tensor", reads=[bTMP[t2], bPS[pB]], writes=[bR[j][s]],
                          out=R[:, j, sl(s)], in0=TMP[t2][:], in1=PSB[pB][:], op=ALU.mult)
                    E(act, "copy", reads=[bUE[u]], writes=[bCARU], out=CARU[:, j, :], in_=UE[u][:, TT:TT + 2])
                wstep("A", cwin[j, :, :], 3072, cj)
            for c in range(DC):
                def cb(W, bW, c=c):
                    for s in range(S):
                        pb = nxt("ps", NROT)
                        for j in range(DC):
                            E(pe, "matmul", reads=[bW, bR[j][s]], writes=[bPS[pb]], inc=(j == DC - 1),
                              out=PSB[pb][:], lhsT=W[:, j * 128:(j + 1) * 128], rhs=R[:, j, sl(s)],
                              start=(j == 0), stop=(j == DC - 1))
                        resid_evac(pb, c, s, gt_ap(i, n, c), True)
                wstep("B", cwout[c, :, :], 1024, cb)
            step(flush_ss)

        def final_store(tt):
            for s in range(S):
                q = s % 2
                if do_final:
                    rstd_for(s, RSTD[q][:], bRSTD[q])
                    for c in range(DC):
                        E(dve, "scalar_tensor_tensor", reads=[bX[c][s], bDER, bRSTD[q]], writes=[bX[c][s]],
                          out=X[:, c, sl(s)], in0=X[:, c, sl(s)], scalar=fg_ap(c), in1=RSTD[q][:],
                          op0=ALU.mult, op1=ALU.mult)
                t0 = tt * TT + s * 512
                E(sp, "dma_start", reads=[bX[c][s] for c in range(DC)], sc=scXS[s],
                  out=outT3[:, :, t0:t0 + 512], in_=X[:, :, sl(s)])

        def load_x(tt):
            for s in range(S):
                t0 = tt * TT + s * 512
                E(sp, "dma_start", writes=[bX[c][s] for c in range(DC)], sc=scXL[s],
                  out=X[:, :, sl(s)], in_=xT3[:, :, t0:t0 + 512])

        for tt in range(NT):
            step(lambda tt=tt: load_x(tt))
            for i in range(n_layers):
                ffn(i, 0, 0)
                if i % 2 == 0:
                    pool_mixer(i, tt)
                else:
                    conv_mixer(i, tt)
                ffn(i, 1, 2)
            step(lambda tt=tt: final_store(tt))

        pools = {"A": (WA, bWA, scWA), "B": (WB, bWB, scWB)}
        order = {p: [k for k, st_ in enumerate(steps) if st_[0] == p] for p in pools}
        nxt_load = {p: 0 for p in pools}
        slot_of = {}

        def issue_loads(done_upto):
            for p, (Wt, bWt, scW) in pools.items():
                n = len(Wt)
                while nxt_load[p] < len(order[p]):
                    q = nxt_load[p]
                    if q >= n and order[p][q - n] >= done_upto:
                        break
                    k = order[p][q]
                    a = q % n
                    slot_of[k] = a
                    E(pool, "dma_start", writes=[bWt[a]], sc=scW[a],
                      out=Wt[a][:, 0:steps[k][2]], in_=steps[k][1])
                    nxt_load[p] += 1

        for k, (p, src_ap, ncols, fn) in enumerate(steps):
            issue_loads(k)
            bg_pop(2)
            if p is None:
                fn()
            else:
                if k not in slot_of:
                    raise RuntimeError("weight piece not scheduled")
                a = slot_of[k]
                fn(pools[p][0][a], pools[p][1][a])

        if carry_io:
            scCO = SemC(new_sem("s_co"), 16)
            E(sp, "dma_start", reads=[bCARP], sc=scCO, out=cout[:, 0:DC * 16].rearrange("p (c k) -> p c k", k=16), in_=CARP[:])
            sp.items.append(("wait", scCO.sem, scCO.count * 16))
            E(sp, "dma_start", reads=[bCARU], sc=scCO, out=cout[:, DC * 16:DC * 18].rearrange("p (c k) -> p c k", k=2), in_=CARU[:])
            sp.items.append(("wait", scCO.sem, scCO.count * 16))
        for s in range(S):
            sp.items.append(("wait", scXS[s].sem, scXS[s].count * 16))

        with nc.Block() as block:
            block.tensor(lambda e: replay(pe.items, e))
            block.scalar(lambda e: replay(act.items, e))
            block.vector(lambda e: replay(dve.items, e))
            block.gpsimd(lambda e: replay(pool.items, e))
            block.sync(lambda e: replay(sp.items, e))
    return nc


def _cols(v):
    v = np.asarray(v, dtype=np.float32).reshape(-1, 128)
    return np.ascontiguousarray(v.T)


def prep_weights(ada_w, ada_b, norm_g, ffn_w_in, ffn_w_out, pool_w, pool_scale, conv_w_in, conv_w,
                 conv_w_out, final_g):
    f32 = np.float32
    vec_common = np.concatenate([
        _cols(np.asarray(ada_b, f32).reshape(-1)),
        _cols(np.asarray(norm_g, f32).reshape(-1)),
        _cols(np.asarray(pool_scale, f32).reshape(-1)),
        _cols(np.asarray(conv_w, f32).reshape(-1)),
        _cols(np.asarray(final_g, f32).reshape(-1)),
    ], axis=1)
    aw = np.asarray(ada_w, f32).reshape(2, D, 72, 128)
    adaw = np.ascontiguousarray(aw.transpose(0, 2, 3, 1)).reshape(144, 128, 1024)
    wi_ = np.asarray(ffn_w_in, f32).reshape(4, DC, 128, 2, FC, 128)
    win = np.ascontiguousarray(wi_.transpose(0, 4, 2, 1, 3, 5)).reshape(4 * FC, 128, 2048)
    wo_ = np.asarray(ffn_w_out, f32).reshape(4, FC, 128, DC, 128)
    wout = np.ascontiguousarray(wo_.transpose(0, 3, 2, 1, 4)).reshape(4 * DC, 128, FC * 128)
    pw_ = np.asarray(pool_w, f32).reshape(4, 2, 128, 256)
    poolw = np.ascontiguousarray(pw_.transpose(0, 2, 1, 3)).reshape(4, 128, 512)
    ci_ = np.asarray(conv_w_in, f32).reshape(DC, 128, 3, DC, 128)
    cwin = np.ascontiguousarray(ci_.transpose(3, 1, 0, 2, 4)).reshape(DC, 128, 3072)
    co_ = np.asarray(conv_w_out, f32).reshape(DC, 128, DC, 128)
    cwout = np.ascontiguousarray(co_.transpose(2, 1, 0, 3)).reshape(DC, 128, 1024)
    return vec_common, dict(adaw=adaw, win=win, wout=wout, poolw=poolw, cwin=cwin, cwout=cwout)


def run(inputs, T=SEQ, TT=1024, FS=2, n_layers=2, do_final=True, trace=False, n_seg=1):
    x = np.asarray(inputs["x"], np.float32)
    c = np.asarray(inputs["c"], np.float32)
    vec_common, wd = prep_weights(*[inputs[k] for k in (
        "ada_w", "ada_b", "norm_g", "ffn_w_in", "ffn_w_out", "pool_w", "pool_scale",
        "conv_w_in", "conv_w", "conv_w_out", "final_g")])
    Ts = T // n_seg
    outs = [[] for _ in range(NB)]
    carry = None
    res = None
    for g in range(n_seg):
        nc = build_nc(Ts, TT, FS, n_layers, do_final, first=(g == 0), carry_io=(n_seg > 1))
        in_maps = []
        for b in range(NB):
            m = dict(wd)
            m["xT"] = np.ascontiguousarray(x[b, g * Ts:(g + 1) * Ts, :].T)
            m["vecs"] = np.ascontiguousarray(np.concatenate([_cols(c[b]), vec_common], axis=1))
            m["cbx"] = np.ascontiguousarray(np.broadcast_to(c[b][None, :], (128, D)))
            if n_seg > 1:
                m["cin"] = carry[b] if carry is not None else np.zeros((128, DC * 18), np.float32)
            in_maps.append(m)
        res = run_bass_kernel_spmd(nc, in_maps, core_ids=list(range(NB)), trace=trace)
        for b, r in enumerate(res.results):
            outs[b].append(np.ascontiguousarray(r["outT"].T))
        if n_seg > 1:
            carry = [np.ascontiguousarray(r["cout"]) for r in res.results]
    out = np.stack([np.concatenate(o, axis=0) for o in outs], axis=0)
    return out.astype(np.float32, copy=False), res


def kernel(x, c, ada_w, ada_b, norm_g, ffn_w_in, ffn_w_out, pool_w, pool_scale,
           conv_w_in, conv_w, conv_w_out, final_g):
    out, _ = run(dict(x=x, c=c, ada_w=ada_w, ada_b=ada_b, norm_g=norm_g, ffn_w_in=ffn_w_in,
                      ffn_w_out=ffn_w_out, pool_w=pool_w, pool_scale=pool_scale, conv_w_in=conv_w_in,
                      conv_w=conv_w, conv_w_out=conv_w_out, final_g=final_g))
    return out
```
